# Optimizing a Trainium2 kernel written in Bass

```python
import math
import jax, jax.numpy as jnp
from jax import lax
import numpy as np

D_MODEL = 1024
BATCH = 4
SEQ = 8192
DEPTH = 2

N_A_LAYERS = DEPTH // 2
N_B_LAYERS = DEPTH - N_A_LAYERS
HG_HEADS = 8
HG_DIM = D_MODEL // HG_HEADS
HG_CHUNK = 32
SW_Q_HEADS = 16
SW_KV_HEADS = 4
SW_HEAD_DIM = D_MODEL // SW_Q_HEADS
SW_GROUP = SW_Q_HEADS // SW_KV_HEADS
SW_WINDOW = 128
REL_BUCKETS = 32
REL_MAX_DIST = 128
FFN_DIM = 2816
CONV_WIDTH = 3
ALPHA = (2.0 * DEPTH) ** 0.25
BETA = (8.0 * DEPTH) ** -0.25
LN_EPS = 1e-5
RMS_EPS = 1e-6

kernel_name = "yoco_hgrn2_swa_sink_convffn"


def layer_norm(x, g, b):
    xf = x.astype(jnp.float32)
    mu = xf.mean(-1, keepdims=True)
    var = jnp.square(xf - mu).mean(-1, keepdims=True)
    y = (xf - mu) * lax.rsqrt(var + LN_EPS) * g.astype(jnp.float32) + b.astype(jnp.float32)
    return y.astype(x.dtype)


def hgrn2_chunkwise(q, k, v, log_f):
    B, S, H, Dk = q.shape
    Dv = v.shape[-1]
    n = S // HG_CHUNK

    def chunks(a):
        return a.reshape(B, n, HG_CHUNK, H, a.shape[-1]).transpose(1, 0, 3, 2, 4)

    qc, kc, vc = chunks(q), chunks(k), chunks(v)
    bc = jnp.cumsum(chunks(log_f), axis=3)
    causal = jnp.tril(jnp.ones((HG_CHUNK, HG_CHUNK), dtype=bool))[:, :, None]

    def step(state, inp):
        q_, k_, v_, b_ = inp
        o_inter = jnp.einsum('bhtk,bhkv->bhtv', q_ * jnp.exp(b_), state)
        diff = b_[:, :, :, None, :] - b_[:, :, None, :, :]
        decay = jnp.where(causal, jnp.exp(jnp.minimum(diff, 0.0)), 0.0)
        scores = jnp.einsum('bhtsk,bhsk->bhts', q_[:, :, :, None, :] * decay, k_)
        o = o_inter + jnp.einsum('bhts,bhsv->bhtv', scores, v_)
        b_last = b_[:, :, -1, :]
        k_dec = k_ * jnp.exp(b_last[:, :, None, :] - b_)
        state = jnp.exp(b_last)[..., None] * state + jnp.einsum('bhsk,bhsv->bhkv', k_dec, v_)
        return state, o

    state0 = jnp.zeros((B, H, Dk, Dv), jnp.float32)
    _, o = lax.scan(step, state0, (qc, kc, vc, bc))
    return o.transpose(1, 0, 3, 2, 4).reshape(B, S, H, Dv)


def hgrn2_mixer(h, w_in, lower_bound, g_norm_w, w_out):
    B, S, D = h.shape
    q, f, i, g = jnp.split(h @ w_in, 4, axis=-1)

    def heads(a):
        return a.reshape(B, S, HG_HEADS, HG_DIM).astype(jnp.float32)

    lb = lower_bound.astype(jnp.float32).reshape(HG_HEADS, HG_DIM)
    fg = lb + (1.0 - lb) * jax.nn.sigmoid(heads(f))
    o = hgrn2_chunkwise(jax.nn.silu(heads(q)), 1.0 - fg, heads(i), jnp.log(fg))
    o = o * lax.rsqrt(jnp.mean(jnp.square(o), -1, keepdims=True) + RMS_EPS)
    o = o * g_norm_w.astype(jnp.float32) * jax.nn.silu(heads(g))
    return o.reshape(B, S, D).astype(h.dtype) @ w_out


def t5_causal_bucket(dist):
    exact = REL_BUCKETS // 2
    d = jnp.maximum(dist, 1).astype(jnp.float32)
    log_b = exact + (jnp.log(d / exact) / math.log(REL_MAX_DIST / exact)
                     * (REL_BUCKETS - exact)).astype(jnp.int32)
    return jnp.where(dist < exact, dist, jnp.minimum(log_b, REL_BUCKETS - 1))


def banded_bias_and_mask(rel_table, n_blocks):
    t = jnp.arange(SW_WINDOW)[:, None] + SW_WINDOW
    s = jnp.arange(2 * SW_WINDOW)[None, :]
    dist = t - s
    bias = rel_table[t5_causal_bucket(jnp.maximum(dist, 0))].transpose(2, 0, 1)
    band = (dist >= 0) & (dist < SW_WINDOW)
    has_prev = (jnp.arange(n_blocks) > 0)[:, None, None]
    mask = band[None] & (has_prev | (s >= SW_WINDOW)[None])
    return bias, mask


def swa_sink_mixer(h, k, v, w_q, sinks, bias, mask, w_out):
    B, S, D = h.shape
    nb = S // SW_WINDOW
    q = (h @ w_q).reshape(B, nb, SW_WINDOW, SW_KV_HEADS, SW_GROUP, SW_HEAD_DIM)

    def with_prev(a):
        ab = a.reshape(B, nb, SW_WINDOW, SW_KV_HEADS, SW_HEAD_DIM)
        prev = jnp.pad(ab, ((0, 0), (1, 0), (0, 0), (0, 0), (0, 0)))[:, :-1]
        return jnp.concatenate([prev, ab], axis=2)

    kk, vv = with_prev(k), with_prev(v)
    scale = SW_HEAD_DIM ** -0.5
    logits = jnp.einsum('bntgrd,bnsgd->bngrts', q, kk).astype(jnp.float32) * scale
    logits = logits + bias.astype(jnp.float32).reshape(SW_KV_HEADS, SW_GROUP, SW_WINDOW, 2 * SW_WINDOW)
    logits = jnp.where(mask[None, :, None, None], logits, -jnp.inf)
    sink = sinks.astype(jnp.float32).reshape(1, 1, SW_KV_HEADS, SW_GROUP, 1)
    m = jnp.maximum(logits.max(-1), sink)
    p = jnp.exp(logits - m[..., None])
    denom = p.sum(-1) + jnp.exp(sink - m)
    o = jnp.einsum('bngrts,bnsgd->bntgrd', p, vv.astype(jnp.float32))
    o = o / jnp.moveaxis(denom, -1, 2)[..., None]
    return o.reshape(B, S, D).astype(h.dtype) @ w_out


def conv_ffn(h, w_in, conv_w, conv_b, w_out):
    u = h @ w_in
    C = u.shape[-1]
    u = lax.conv_general_dilated(u, conv_w[:, None, :], window_strides=(1,),
                                 padding=[(CONV_WIDTH - 1, 0)],
                                 dimension_numbers=('NWC', 'WIO', 'NWC'),
                                 feature_group_count=C) + conv_b
    a, b = jnp.split(u, 2, axis=-1)
    return (jax.nn.silu(a) * b) @ w_out


def setup_inputs(seed: int = 0) -> dict:
    key = jax.random.key(seed)
    ks = jax.random.split(key, 24)
    D, F = D_MODEL, FFN_DIM
    kv_dim = SW_KV_HEADS * SW_HEAD_DIM
    nrm = jax.random.normal

    x = nrm(ks[0], (BATCH, SEQ, D), jnp.float32)

    hgrn_w_in = nrm(ks[1], (N_A_LAYERS, D, 4 * D), jnp.float32) * D ** -0.5
    hgrn_w_in = hgrn_w_in.at[..., 2 * D:3 * D].multiply(BETA)
    hgrn_lb_logits = nrm(ks[2], (N_A_LAYERS + 1, D), jnp.float32) * 0.5
    hgrn_gnorm_w = 1.0 + 0.02 * nrm(ks[3], (N_A_LAYERS, HG_DIM), jnp.float32)
    hgrn_w_out = nrm(ks[4], (N_A_LAYERS, D, D), jnp.float32) * D ** -0.5 * BETA

    swa_w_q = nrm(ks[5], (N_B_LAYERS, D, D), jnp.float32) * D ** -0.5
    swa_sinks = nrm(ks[6], (N_B_LAYERS, SW_Q_HEADS), jnp.float32) * 0.5
    swa_w_out = nrm(ks[7], (N_B_LAYERS, D, D), jnp.float32) * D ** -0.5 * BETA
    shared_w_kv = nrm(ks[8], (D, 2 * kv_dim), jnp.float32) * D ** -0.5
    shared_w_kv = shared_w_kv.at[:, kv_dim:].multiply(BETA)
    rel_bias = nrm(ks[9], (REL_BUCKETS, SW_Q_HEADS), jnp.float32) * 0.5

    ffn_w_in = nrm(ks[10], (DEPTH, D, 2 * F), jnp.float32) * D ** -0.5 * BETA
    ffn_conv_w = nrm(ks[11], (DEPTH, CONV_WIDTH, 2 * F), jnp.float32) * CONV_WIDTH ** -0.5
    ffn_conv_b = nrm(ks[12], (DEPTH, 2 * F), jnp.float32) * 0.02
    ffn_w_out = nrm(ks[13], (DEPTH, F, D), jnp.float32) * F ** -0.5 * BETA

    ln_mix_g = 1.0 + 0.02 * nrm(ks[14], (DEPTH, D), jnp.float32)
    ln_mix_b = 0.02 * nrm(ks[15], (DEPTH, D), jnp.float32)
    ln_ffn_g = 1.0 + 0.02 * nrm(ks[16], (DEPTH, D), jnp.float32)
    ln_ffn_b = 0.02 * nrm(ks[17], (DEPTH, D), jnp.float32)

    return {"x": x, "hgrn_w_in": hgrn_w_in, "hgrn_lb_logits": hgrn_lb_logits,
            "hgrn_gnorm_w": hgrn_gnorm_w, "hgrn_w_out": hgrn_w_out,
            "swa_w_q": swa_w_q, "swa_sinks": swa_sinks, "swa_w_out": swa_w_out,
            "shared_w_kv": shared_w_kv, "rel_bias": rel_bias,
            "ffn_w_in": ffn_w_in, "ffn_conv_w": ffn_conv_w, "ffn_conv_b": ffn_conv_b,
            "ffn_w_out": ffn_w_out, "ln_mix_g": ln_mix_g, "ln_mix_b": ln_mix_b,
            "ln_ffn_g": ln_ffn_g, "ln_ffn_b": ln_ffn_b}


def reference(x, hgrn_w_in, hgrn_lb_logits, hgrn_gnorm_w, hgrn_w_out,
              swa_w_q, swa_sinks, swa_w_out, shared_w_kv, rel_bias,
              ffn_w_in, ffn_conv_w, ffn_conv_b, ffn_w_out,
              ln_mix_g, ln_mix_b, ln_ffn_g, ln_ffn_b):
    B, S, D = x.shape
    n_blocks = S // SW_WINDOW
    lower_bounds = jnp.cumsum(jax.nn.softmax(hgrn_lb_logits.astype(jnp.float32), axis=0), axis=0)
    bias, mask = banded_bias_and_mask(rel_bias, n_blocks)

    h = x
    k_shared = v_shared = None
    for layer in range(DEPTH):
        if layer < N_A_LAYERS:
            mix = hgrn2_mixer(h, hgrn_w_in[layer], lower_bounds[layer],
                              hgrn_gnorm_w[layer], hgrn_w_out[layer])
        else:
            j = layer - N_A_LAYERS
            mix = swa_sink_mixer(h, k_shared, v_shared, swa_w_q[j], swa_sinks[j],
                                 bias, mask, swa_w_out[j])
        h = layer_norm(ALPHA * h + mix, ln_mix_g[layer], ln_mix_b[layer])
        ff = conv_ffn(h, ffn_w_in[layer], ffn_conv_w[layer], ffn_conv_b[layer], ffn_w_out[layer])
        h = layer_norm(ALPHA * h + ff, ln_ffn_g[layer], ln_ffn_b[layer])
        if layer == N_A_LAYERS - 1:
            k_flat, v_flat = jnp.split(h @ shared_w_kv, 2, axis=-1)
            k_shared = k_flat.reshape(B, S, SW_KV_HEADS, SW_HEAD_DIM)
            v_shared = v_flat.reshape(B, S, SW_KV_HEADS, SW_HEAD_DIM)
    return h
```

```python
import math
from contextlib import ExitStack
import numpy as np
import concourse.bass as bass
import concourse.mybir as mybir
from concourse.bass_utils import run_bass_kernel_spmd

F32 = mybir.dt.float32
BF16 = mybir.dt.bfloat16
AF = mybir.ActivationFunctionType
ALU = mybir.AluOpType
AX = mybir.AxisListType

D = 1024
FF = 2816
ALPHA = (2.0 * 2) ** 0.25
LN_EPS = 1e-5
RMS_EPS = 1e-6
NEG = -30000.0
ENGS = ["tensor", "vector", "scalar", "gpsimd", "sync"]


class Buf:
    __slots__ = ("name", "last_write", "reads")

    def __init__(self, name=""):
        self.name = name
        self.last_write = None
        self.reads = []


class Instr:
    __slots__ = ("eng", "fn", "deps", "is_dma", "sem", "val", "signals", "key", "ndma")

    def __init__(self, eng, fn, key=None, ndma=1):
        self.eng = eng
        self.fn = fn
        self.deps = []
        self.is_dma = key is not None
        self.key = key
        self.ndma = ndma
        self.sem = None
        self.val = None
        self.signals = self.is_dma


class Prog:
    def __init__(self, nc):
        self.nc = nc
        self.streams = {e: [] for e in ENGS}

    def op(self, eng, fn, reads=(), writes=(), dma_key=None, ndma=1):
        ins = Instr(eng, fn, key=dma_key, ndma=ndma)
        deps = []
        for b in list(reads) + list(writes):
            if b.last_write is not None:
                deps.append(b.last_write)
        for b in writes:
            deps.extend(b.reads)
        seen = set()
        for d in deps:
            if d is ins or id(d) in seen:
                continue
            if d.eng == "tensor" and eng == "tensor" and not d.is_dma and not ins.is_dma:
                continue
            if d.is_dma and ins.is_dma and d.key == ins.key and str(d.key).startswith("G"):
                continue
            seen.add(id(d))
            ins.deps.append(d)
        for b in writes:
            b.last_write = ins
            b.reads = []
        for b in reads:
            if b not in writes:
                b.reads.append(ins)
        self.streams[eng].append(ins)
        return ins

    def emit(self, stack):
        nc = self.nc
        for e in ENGS:
            for ins in self.streams[e]:
                for d in ins.deps:
                    d.signals = True
        eng_sem = {e: stack.enter_context(nc.semaphore("s_" + e)) for e in ENGS}
        dma_sem, dma_cnt, key_eng = {}, {}, {}
        for e in ENGS:
            cnt = 0
            for ins in self.streams[e]:
                if ins.is_dma:
                    k = ins.key
                    if k not in dma_sem:
                        dma_sem[k] = stack.enter_context(nc.semaphore("d_" + str(k)))
                        dma_cnt[k] = 0
                        key_eng[k] = e
                    assert key_eng[k] == e
                    dma_cnt[k] += 16 * ins.ndma
                    ins.val = dma_cnt[k]
                    ins.sem = dma_sem[k]
                elif ins.signals:
                    cnt += 1
                    ins.val = cnt
                    ins.sem = eng_sem[e]
        for e in ENGS:
            for ins in self.streams[e]:
                if ins.is_dma and str(ins.key).startswith("G"):
                    ins.val = dma_cnt[ins.key]
        block = stack.enter_context(nc.Block())
        stats = {}

        def make(e):
            def body(engobj):
                seen = {}
                nw = 0
                for ins in self.streams[e]:
                    for d in ins.deps:
                        key = id(d.sem)
                        if seen.get(key, 0) >= d.val:
                            continue
                        seen[key] = d.val
                        engobj.wait_ge(d.sem, d.val)
                        nw += 1
                    if ins.fn is None:
                        continue
                    r = ins.fn(engobj)
                    if ins.is_dma:
                        if isinstance(r, (list, tuple)):
                            assert len(r) == ins.ndma
                            for x in r:
                                x.then_inc(ins.sem, 16)
                        else:
                            r.then_inc(ins.sem, 16)
                    elif ins.signals:
                        r.then_inc(ins.sem, 1)
                stats[e] = (len(self.streams[e]), nw)
            return body

        for e in ENGS:
            getattr(block, e)(make(e))
        return stats


class Ring:
    def __init__(self, items):
        self.items = items
        self.i = 0

    def next(self):
        x = self.items[self.i % len(self.items)]
        self.i += 1
        return x


def build(S, dbg=False):
    Tm = S // 2
    NW = 2
    NMAIN = Tm // 128
    NT_ALL = NMAIN + NW
    NPRE = (Tm - 256) // 128
    assert NMAIN % 4 == 0 and NPRE >= 0
    groups = [(0, 2)] + [(2 + 4 * i, 4) for i in range(NMAIN // 4)]
    pre_groups = []
    t = 0
    while t < NPRE:
        n = min(4, NPRE - t)
        pre_groups.append((t, n))
        t += n

    nc = bass.Bass("TRN2", target_bir_lowering=False)
    P = Prog(nc)

    def din(name, shape, dt=F32):
        return nc.dram_tensor(name, list(shape), dt, kind="ExternalInput").ap()

    x_main = din("x_main", [NT_ALL * 128, D])
    x_pre = din("x_pre", [max(NPRE, 1) * 128, D])
    w_hin = din("w_hin", [D, 4 * D])
    w_hout = din("w_hout", [D, D])
    w_q = din("w_q", [D, D])
    w_o = din("w_o", [D, D])
    w_kv = din("w_kv", [D, 512])
    w_fin = din("w_fin", [2, D, 2 * FF])
    w_fout = din("w_fout", [2, FF, D])
    lbl = din("lbl", [128, 2, 8])
    gw_row = din("gw_row", [1, D])
    lnp_d = din("lnp", [8, D])
    cw_d = din("cw", [128, 2, 44, 4])
    sink_row = din("sink_row", [1, 16])
    relg = din("relg", [128, 16, 256])
    maskadd_d = din("maskadd", [128, 256])
    flag_d = din("flag", [128, 1])
    ident_d = din("ident", [128, 128])
    mask01_d = din("mask01", [128, 512])
    scanm_d = din("scanm", [128, 512])
    cmask_d = din("cmask", [128, 512])
    out_d = nc.dram_tensor("out", [Tm, D], F32, kind="ExternalOutput").ap()
    import os as _os2
    DBGTAP = _os2.environ.get("DBGTAP") == "1"
    if DBGTAP:
        dbg_d = nc.dram_tensor("dbgtap", [3, 4, 128, D], F32, kind="ExternalOutput").ap()
        dbg_o = nc.dram_tensor("dbgo", [128, 4, 512], F32, kind="ExternalOutput").ap()

    def dscr(name, shape):
        return nc.dram_tensor(name, list(shape), BF16, kind="Internal").ap()

    wb_hin = dscr("wb_hin", [D, 4 * D])
    wb_hout = dscr("wb_hout", [D, D])
    wb_q = dscr("wb_q", [D, D])
    wb_o = dscr("wb_o", [D, D])
    wb_kv = dscr("wb_kv", [D, 768])
    wb_fin = dscr("wb_fin", [2, D, 11, 512])
    wb_fout = dscr("wb_fout", [2, 2, FF, 512])
    B_wb = {k: [Buf(k + "_0"), Buf(k + "_1")] for k in ["hin_fi", "hin_qg", "hout", "q", "o", "kv", "fin0", "fin1", "fout0", "fout1"]}

    with ExitStack() as st:
        def sb(name, shape, dt):
            return st.enter_context(nc.sbuf_tensor(name, list(shape), dt))

        h_res = sb("h_res", [128, 4, D], F32)
        B_h = [Buf("h%d" % i) for i in range(4)]
        hT = sb("hT", [128, 8, 512], BF16)
        B_hT = [Buf("hT%d" % i) for i in range(4)]
        gT = sb("gT", [128, 22, 512], BF16)
        B_gT = [Buf("gT%d" % j) for j in range(22)]
        NS = 3
        ring_t = [sb("ring%d" % i, [128, 4096], BF16) for i in range(NS)]
        B_ring = [Buf("ring%d" % i) for i in range(NS)]
        ring = Ring(list(range(NS)))
        lnp = sb("lnp_sb", [128, 2, D], F32)
        B_lnp = Buf("lnp")
        bias_tab = sb("bias_tab", [128, 16, 256], F32)
        B_bias = Buf("bias")
        ident_f = sb("ident_f", [128, 128], F32)
        ident_b = sb("ident_b", [128, 128], BF16)
        mask01 = sb("mask01_sb", [128, 512], F32)
        scanm = sb("scanm_sb", [128, 512], F32)
        cmask_f = sb("cmask_f", [128, 512], F32)
        cmask = sb("cmask_b", [128, 4, 128], BF16)
        gw_bc = sb("gw_bc", [128, D], F32)
        maskadd = sb("maskadd_sb", [128, 256], F32)
        sink_bc = sb("sink_bc", [128, 16], F32)
        flag = sb("flag_sb", [128, 1], F32)
        negflag = sb("negflag", [128, 1], F32)
        lb_t = sb("lb_t", [128, 2, 8], F32)
        lb = sb("lb", [128, 8], F32)
        oml = sb("oml", [128, 8], F32)
        cw = sb("cw_sb", [128, 2, 44, 4], F32)
        B_const = Buf("const")
        S32 = sb("S32", [128, 8, 128], F32)
        S_bf_l = [sb("S_bf%d" % h_, [128, 128], BF16) for h_ in range(8)]

        class _SB:
            def __getitem__(self, key):
                return S_bf_l[key[1]][:, :]
        S_bf = _SB()
        B_S32 = [Buf("S32_%d" % h) for h in range(8)]
        B_Sbf = [Buf("Sbf_%d" % h) for h in range(8)]
        KT = sb("KT", [128, 4, 640], BF16)
        B_KT = Buf("KT")
        V_tok = sb("V_tok", [128, 5, 256], BF16)
        B_V = Buf("V")
        halo = sb("halo", [128, 2, 44, 2], F32)
        B_halo = [Buf("halo0"), Buf("halo1")]
        xb = [sb("xb%d" % i, [128, D], BF16) for i in range(2)]
        B_xb = [Buf("xb0"), Buf("xb1")]
        xb_ring = Ring([0, 1])
        small = sb("small", [128, 256], F32)
        B_small = Buf("small")
        ebl_all = sb("ebl_all", [128, 4, 16], F32)
        B_ebl = [Buf("ebl%d" % j) for j in range(4)]
        junk_t = sb("junk_t", [128, 128], F32)
        B_junk = Buf("junk")
        B_hst = Buf("hst")
        B_ast = Buf("ast")
        bnst = sb("bnst", [128, 4, 2, 6], F32)
        B_bnst = [Buf("bnst%d" % i) for i in range(4)]
        mv = sb("mv", [128, 4, 2], F32)
        lnr = sb("lnr", [128, 8], F32)
        B_mv = Buf("mv")

        XB = 78 * 1024
        xa = sb("xarena", [128, XB // 2], BF16)
        NSLOT = XB // 1024
        B_x = [Buf("xs%d" % i) for i in range(NSLOT)]

        class Arena:
            def __init__(self):
                self.off = 0

            def reset(self):
                self.off = 0

            def alloc(self, shape, dt):
                n = 1
                for s_ in shape[1:]:
                    n *= s_
                nbytes = n * (4 if dt == F32 else 2)
                nb = (nbytes + 1023) // 1024 * 1024
                assert self.off + nb <= XB, ("arena overflow", self.off, nb)
                v = xa[:, self.off // 2:(self.off + nbytes) // 2]
                if dt == F32:
                    v = v.bitcast(F32)
                names = "abcdefg"[:len(shape) - 1]
                if len(shape) > 2:
                    pat = "p (" + " ".join(names) + ") -> p " + " ".join(names)
                    kw = {names[k]: shape[k + 1] for k in range(1, len(shape) - 1)}
                    v = v.rearrange(pat, **kw)
                bufs = B_x[self.off // 1024:(self.off + nb) // 1024]
                self.off += nb
                return v, bufs

        ar = Arena()

        pb = [st.enter_context(nc.psum_tensor("pb%d" % i, [128, 512], F32)) for i in range(8)]
        B_pb = [Buf("pb%d" % i) for i in range(8)]
        pf_ring = Ring([0, 1])
        po_ring = Ring([2, 3])
        PT_BANK = 4

        def pbT(i):
            return pb[i][:].bitcast(BF16).rearrange("p (a b) -> p a b", b=128)

        def mm(out, lhsT, rhs, start, stop, R, W, skip=False, tp=None):
            P.op("tensor", lambda e: e.matmul(out, lhsT=lhsT, rhs=rhs, start=start, stop=stop,
                                              skip_group_check=skip, tile_position=tp), reads=R, writes=W)

        def tr(out, in_, R, W):
            P.op("tensor", lambda e: e.transpose(out=out, in_=in_, identity=ident_b[:]), reads=R + [B_const], writes=W)

        def act(out, in_, func, R, W, bias=None, scale=None, accum=None):
            kw = {}
            if bias is not None:
                kw["bias"] = bias
            if scale is not None:
                kw["scale"] = scale
            if accum is not None:
                kw["accum_out"] = accum
            P.op("scalar", lambda e: e.activation(out=out, in_=in_, func=func, **kw), reads=R, writes=W)

        def tt(eng, out, in0, in1, op, R, W):
            P.op(eng, lambda e: e.tensor_tensor(out=out, in0=in0, in1=in1, op=op), reads=R, writes=W)

        def ts(eng, out, in0, s1, s2, op0, op1, R, W):
            if op1 is None:
                P.op(eng, lambda e: e.tensor_scalar(out=out, in0=in0, scalar1=s1, scalar2=None, op0=op0), reads=R, writes=W)
            else:
                P.op(eng, lambda e: e.tensor_scalar(out=out, in0=in0, scalar1=s1, scalar2=s2, op0=op0, op1=op1), reads=R, writes=W)

        def stt(eng, out, in0, scalar, in1, op0, op1, R, W):
            P.op(eng, lambda e: e.scalar_tensor_tensor(out=out, in0=in0, scalar=scalar, in1=in1, op0=op0, op1=op1), reads=R, writes=W)

        def cp(eng, out, in_, R, W):
            P.op(eng, lambda e: e.tensor_copy(out=out, in_=in_), reads=R, writes=W)

        dma_uid = [0]

        def dma(out, in_, R, W, key=None, eng="sync"):
            if key is None:
                dma_uid[0] += 1
                key = "u%d" % dma_uid[0]
            P.op(eng, lambda e: e.dma_start(out=out, in_=in_), reads=R, writes=W, dma_key=key)

        import os as _os
        KC = int(_os.environ.get("KC", "0"))
        def cdma(out, in_):
            if KC & 16:
                return
            dma(out, in_, [], [B_const], key="Gconst")

        cdma(ident_f[:], ident_d)
        cdma(mask01[:], mask01_d)
        cdma(scanm[:], scanm_d)
        cdma(cmask_f[:], cmask_d)
        if not KC & 1:
            cdma(gw_bc[:], gw_row[0, :].partition_broadcast(128))
        cdma(maskadd[:], maskadd_d)
        if not KC & 1:
            cdma(sink_bc[:], sink_row[0, :].partition_broadcast(128))
        cdma(flag[:], flag_d)
        cdma(lb_t[:], lbl)
        cdma(cw[:], cw_d)
        if not KC & 32:
            dma(bias_tab[:], relg, [], [B_bias], key="Gconst")
        if not KC & 64:
            cp("vector", ident_b[:], ident_f[:], [B_const], [B_const])
        cp("vector", cmask[:].rearrange("p a b -> p (a b)"), cmask_f[:], [B_const], [B_const])
        if not KC & 8:
            tt("vector", lb[:], lb_t[:, 0, :], lb_t[:, 1, :], ALU.subtract, [B_const], [B_const])
            act(lb[:], lb[:], AF.Sigmoid, [B_const], [B_const])
            ts("vector", oml[:], lb[:], -1.0, 1.0, ALU.mult, ALU.add, [B_const], [B_const])
            ts("vector", negflag[:], flag[:], -NEG, NEG, ALU.mult, ALU.add, [B_const], [B_const])
        if not KC & 2:
            tt("vector", bias_tab[:], bias_tab[:], maskadd[:].unsqueeze(1).to_broadcast([128, 16, 256]), ALU.add,
               [B_bias, B_const], [B_bias])
        if not KC & 4:
            P.op("gpsimd", lambda e: e.memset(S32[:], 0.0), writes=B_S32)
            P.op("gpsimd", lambda e: e.memset(KT[:], 0.0), writes=[B_KT])
            P.op("gpsimd", lambda e: e.memset(V_tok[:], 0.0), writes=[B_V])
            P.op("gpsimd", lambda e: e.memset(halo[:], 0.0), writes=B_halo)

        gT_f = gT[:].rearrange("p a b -> p (a b)")
        stg = [gT_f[:, 0:4096].bitcast(F32), gT_f[:, 4096:8192].bitcast(F32)]
        lnp_b = lnp[:].rearrange("p a b -> p (a b)").bitcast(BF16)
        stb = [lnp_b[:, 0:2048], lnp_b[:, 2048:4096]]
        B_stg = [Buf("stg0"), Buf("stg1")]
        B_stb = [Buf("stb0"), Buf("stb1")]
        cv_i = [0]

        def conv_unit(src, ncols, stores, wbuf):
            s = cv_i[0] % 2
            cv_i[0] += 1
            dma(stg[s][:, :ncols], src, [], [B_stg[s]], key="cvl%d" % s)
            cp("gpsimd", stb[s][:, :ncols], stg[s][:, :ncols], [B_stg[s]], [B_stb[s]])
            n = len(stores)

            def fn(e, s=s):
                return [e.dma_start(out=d, in_=f(stb[s])) for d, f in stores]
            P.op("sync", fn, reads=[B_stb[s]], writes=[wbuf[s]], dma_key="cvs%d" % s, ndma=n)

        def conv_units():
            for r in range(8):
                rows = slice(r * 128, (r + 1) * 128)
                yield lambda rows=rows: conv_unit(w_hin[rows, 1024:3072], 2048,
                                                  [(wb_hin[rows, 1024:3072], lambda t_: t_[:, :2048])], B_wb["hin_fi"])
            for r in range(8):
                rows = slice(r * 128, (r + 1) * 128)
                yield lambda rows=rows: conv_unit(w_hin[rows, 0:1024], 1024,
                                                  [(wb_hin[rows, 0:1024], lambda t_: t_[:, :1024])], B_wb["hin_qg"])
                yield lambda rows=rows: conv_unit(w_hin[rows, 3072:4096], 1024,
                                                  [(wb_hin[rows, 3072:4096], lambda t_: t_[:, :1024])], B_wb["hin_qg"])
            for r in range(8):
                rows = slice(r * 128, (r + 1) * 128)
                yield lambda rows=rows: conv_unit(w_hout[rows, :], 1024,
                                                  [(wb_hout[rows, :], lambda t_: t_[:, :1024])], B_wb["hout"])
            for l in range(2):
                if l == 1:
                    for r in range(8):
                        rows = slice(r * 128, (r + 1) * 128)
                        yield lambda rows=rows: conv_unit(
                            w_kv[rows, :], 512,
                            [(wb_kv[rows, 0:256], lambda t_: t_[:, 0:256]),
                             (wb_kv[rows, 512:768], lambda t_: t_[:, 256:512]),
                             (wb_kv[rows, 256:512].rearrange("p (a b c) -> p a b c", a=2, b=2)[:, :, 0, :],
                              lambda t_: t_[:, 0:256].rearrange("p (a b c) -> p a b c", a=2, b=2)[:, :, 1, :]),
                             (wb_kv[rows, 256:512].rearrange("p (a b c) -> p a b c", a=2, b=2)[:, :, 1, :],
                              lambda t_: t_[:, 0:256].rearrange("p (a b c) -> p a b c", a=2, b=2)[:, :, 0, :])],
                            B_wb["kv"])
                    for nm, wsrc, wdst in (("q", w_q, wb_q), ("o", w_o, wb_o)):
                        for r in range(8):
                            rows = slice(r * 128, (r + 1) * 128)
                            yield lambda rows=rows, wsrc=wsrc, wdst=wdst, nm=nm: conv_unit(
                                wsrc[rows, :], 1024, [(wdst[rows, :], lambda t_: t_[:, :1024])], B_wb[nm])
                for r in range(8):
                    rows = slice(r * 128, (r + 1) * 128)
                    for part in range(2):
                        for (p0, p1) in ((0, 6), (6, 11)):
                            c0 = part * FF + p0 * 256
                            ncol = (p1 - p0) * 256
                            yield lambda rows=rows, l=l, part=part, p0=p0, p1=p1, c0=c0, ncol=ncol: conv_unit(
                                w_fin[l, rows, c0:c0 + ncol], ncol,
                                [(wb_fin[l, rows, p0:p1, part * 256:(part + 1) * 256],
                                  lambda t_, ncol=ncol: t_[:, :ncol].rearrange("p (a b) -> p a b", b=256))],
                                B_wb["fin%d" % l])
                for r in range(22):
                    rows = slice(r * 128, (r + 1) * 128)
                    yield lambda rows=rows, l=l: conv_unit(
                        w_fout[l, rows, :], 1024,
                        [(wb_fout[l, :, rows, :].rearrange("h p c -> p h c"),
                          lambda t_: t_[:, :1024].rearrange("p (h c) -> p h c", h=2))],
                        B_wb["fout%d" % l])

        cv_gen = conv_units()

        def conv_some(n):
            for _ in range(n):
                u = next(cv_gen, None)
                if u is None:
                    return False
                u()
            return True

        def load_piece(src_ap, nk, ncols, wbuf):
            s = ring.next()
            view = ring_t[s][:, :nk * ncols].rearrange("p (k c) -> p k c", c=ncols)
            dma(view, src_ap.rearrange("(k p) c -> p k c", p=128), list(wbuf), [B_ring[s]], key="ring%d" % s)
            return view, B_ring[s]

        def to_hT(i, src_bf, Bsrc, dst, Bdst):
            pt = pbT(PT_BANK)
            for kc in range(8):
                tr(pt[:, kc, :], src_bf[:, kc * 128:(kc + 1) * 128], [Bsrc], [B_pb[PT_BANK]])
            act(dst[:, 0:8, i * 128:(i + 1) * 128], pt, AF.Copy, [B_pb[PT_BANK]], Bdst)

        def load_x(src, t0, nt):
            for i in range(nt):
                dma(h_res[:, i, :], src[(t0 + i) * 128:(t0 + i + 1) * 128, :], [], [B_h[i]], key="xin%d" % i)
                s = xb_ring.next()
                cp("vector", xb[s][:], h_res[:, i, :], [B_h[i]], [B_xb[s]])
                to_hT(i, xb[s], B_xb[s], hT, [B_hT[i]])

        def mkap(view, off, dims):
            return bass.AP(tensor=view.tensor, offset=view.offset + off, ap=[list(view.ap[0])] + [list(d) for d in dims])

        def hgrn_group(nt, with_q):
            N = nt * 128
            nch = nt * 4
            ar.reset()
            qs, Bqs = ar.alloc([128, 512], F32)
            fg, Bfg = ar.alloc([128, 512], F32)
            lf, Blf = ar.alloc([128, 512], F32)
            bc, Bbc = ar.alloc([128, 512], F32)
            eb, Beb = ar.alloc([128, 512], F32)
            enb, Benb = ar.alloc([128, 512], F32)
            kt, Bkt = ar.alloc([128, 512], BF16)
            kdT, BkdT = ar.alloc([128, 512], BF16)
            v_tok, Bv = ar.alloc([128, 4, 512], BF16)
            ph = []
            for j in range(4):
                d_ = {}
                d_["kd"], d_["Bkd"] = ar.alloc([128, 4, 128], BF16)
                d_["ebl"], d_["Bebl"] = ebl_all[:, j, :], [B_ebl[j]]
                if with_q:
                    d_["kdz"] = [ar.alloc([128, 4, 128], BF16) for _ in range(2)]
                    d_["qpad"], d_["Bqpad"] = ar.alloc([128, 16, 128], BF16)
                    d_["sm"], d_["Bsm"] = ar.alloc([128, 512], BF16)
                ph.append(d_)
            if with_q:
                gs, Bgs = ar.alloc([128, 4, 512], F32)
                o_sb, Bo = ar.alloc([128, 4, 512], F32)
                t1, Bt1 = ar.alloc([128, 512], F32)
                gated_l = [ar.alloc([128, 4, 512], BF16) for _ in range(2)]
                junk, Bjunk = junk_t, [B_junk]
                hst, Bhst = small[:, 0:64], [B_hst]
                for j in range(4):
                    P.op("gpsimd", lambda e, j=j: e.memset(ph[j]["qpad"], 0.0), writes=ph[j]["Bqpad"])
            import os as _os3
            _hgs = [int(_os3.environ["HGONLY"])] if "HGONLY" in _os3.environ else [0, 1]
            _wfix = _os3.environ.get("WHG0") == "1"
            for hg_real in _hgs:
                hg = 0 if _wfix else hg_real
                Wi, BWi = load_piece(wb_hin[:, 2048 + hg * 512:2048 + (hg + 1) * 512], 8, 512, B_wb["hin_fi"])
                if with_q:
                    Wg, BWg = load_piece(wb_hin[:, 3072 + hg * 512:3072 + (hg + 1) * 512], 8, 512, B_wb["hin_qg"])
                for i in range(nt):
                    b_ = pf_ring.next()
                    for kc in range(8):
                        mm(pb[b_][:, :], hT[:, kc, i * 128:(i + 1) * 128], Wi[:, kc, :], kc == 0, kc == 7,
                           [B_hT[i], BWi], [B_pb[b_]])
                    act(v_tok[:, i, :], pb[b_][:, :], AF.Copy, [B_pb[b_]], Bv)
                    if with_q:
                        b_ = pf_ring.next()
                        for kc in range(8):
                            mm(pb[b_][:, :], hT[:, kc, i * 128:(i + 1) * 128], Wg[:, kc, :], kc == 0, kc == 7,
                               [B_hT[i], BWg], [B_pb[b_]])
                        act(gs[:, i, :], pb[b_][:, :], AF.Silu, [B_pb[b_]], Bgs)
                        tt("vector", gs[:, i, :], gs[:, i, :], gw_bc[:, hg * 512:(hg + 1) * 512], ALU.mult,
                           Bgs + [B_const], Bgs)
                if with_q:
                    ck(4.2)
                if with_q:
                    Wq, BWq = load_piece(wb_hin[:, hg * 512:(hg + 1) * 512], 8, 512, B_wb["hin_qg"])
                Wf, BWf = load_piece(wb_hin[:, 1024 + hg * 512:1024 + (hg + 1) * 512], 8, 512, B_wb["hin_fi"])
                for j in range(4):
                    h = hg_real * 4 + j
                    pj = ph[j]
                    if with_q:
                        b_ = pf_ring.next()
                        for kc in range(8):
                            mm(pb[b_][:, :N], Wq[:, kc, j * 128:(j + 1) * 128], hT[:, kc, :N], kc == 0, kc == 7,
                               B_hT[:nt] + [BWq], [B_pb[b_]])
                        act(qs[:, :N], pb[b_][:, :N], AF.Silu, [B_pb[b_]], Bqs)
                    b_ = pf_ring.next()
                    for kc in range(8):
                        mm(pb[b_][:, :N], Wf[:, kc, j * 128:(j + 1) * 128], hT[:, kc, :N], kc == 0, kc == 7,
                           B_hT[:nt] + [BWf], [B_pb[b_]])
                    act(fg[:, :N], pb[b_][:, :N], AF.Sigmoid, [B_pb[b_]], Bfg)
                    ts("vector", fg[:, :N], fg[:, :N], oml[:, h:h + 1], lb[:, h:h + 1], ALU.mult, ALU.add,
                       Bfg + [B_const], Bfg)
                    act(lf[:, :N], fg[:, :N], AF.Ln, Bfg, Blf)
                    P.op("vector", lambda e, N=N: e.tensor_tensor_scan(out=bc[:, :N], data0=scanm[:, :N], data1=lf[:, :N],
                                                                      initial=0.0, op0=ALU.mult, op1=ALU.add),
                         reads=Blf + [B_const], writes=Bbc)
                    act(eb[:, :N], bc[:, :N], AF.Exp, Bbc, Beb)
                    act(enb[:, :N], bc[:, :N], AF.Exp, Bbc, Benb, scale=-1.0)
                    ts("vector", fg[:, :N], fg[:, :N], -1.0, 1.0, ALU.mult, ALU.add, Bfg, Bfg)
                    ebv = eb[:, :N].rearrange("p (c t) -> p c t", t=32)
                    cp("vector", pj["ebl"][:, :nch], ebv[:, :, 31], Beb, pj["Bebl"])
                    if with_q:
                        qd = mkap(pj["qpad"], 0, [[512, nt], [160, 4], [1, 32]])
                        tt("vector", qd, qs[:, :N].rearrange("p (a b c) -> p a b c", b=4, c=32),
                           eb[:, :N].rearrange("p (a b c) -> p a b c", b=4, c=32), ALU.mult,
                           Bqs + Beb, pj["Bqpad"])
                    tt("vector", kt[:, :N], fg[:, :N], enb[:, :N], ALU.mult, Bfg + Benb, Bkt)
                    tt("vector", kdT[:, :N].rearrange("p (c t) -> p c t", t=32),
                       kt[:, :N].rearrange("p (c t) -> p c t", t=32),
                       pj["ebl"][:, :nch].unsqueeze(2).to_broadcast([128, nch, 32]), ALU.mult,
                       Bkt + pj["Bebl"], BkdT)
                    if with_q and nt == 4 and hg == 0 and j == 0:
                        tapbuf("t_qs", qs[:, :], Bqs, [128, 512], F32)
                        tapbuf("t_bc", bc[:, :], Bbc, [128, 512], F32)
                        tapbuf("t_km", fg[:, :], Bfg, [128, 512], F32)
                        tapbuf("t_kt", kt[:, :], Bkt, [128, 512], BF16)
                        tapbuf("t_kdT", kdT[:, :], BkdT, [128, 512], BF16)
                        tapbuf("t_qpad", pj["qpad"], pj["Bqpad"], [128, 16, 128], BF16)
                    pt = pbT(PT_BANK)
                    for i in range(nt):
                        tr(pt[:, i, :], kdT[:, i * 128:(i + 1) * 128], BkdT, [B_pb[PT_BANK]])
                    act(pj["kd"][:, :nt, :], pt[:, :nt, :], AF.Copy, [B_pb[PT_BANK]], pj["Bkd"])
                    if with_q:
                        for i in range(nt):
                            qdi = mkap(pj["qpad"], i * 512, [[160, 4], [1, 32]])
                            mm(pb[5][:, i * 128:(i + 1) * 128], kt[:, i * 128:(i + 1) * 128], qdi, True, True,
                               Bkt + pj["Bqpad"], [B_pb[5]])
                        tt("vector", pj["sm"][:, :N], pb[5][:, :N], mask01[:, :N], ALU.mult,
                           [B_pb[5], B_const], pj["Bsm"])
                        if nt == 4 and hg == 0 and j == 0:
                            tapbuf("t_sm", pj["sm"], pj["Bsm"], [128, 512], BF16)
                            tapbuf("t_kd", pj["kd"], pj["Bkd"], [128, 4, 128], BF16)
                            tapbuf("t_v", v_tok, Bv, [128, 4, 512], BF16)
                if with_q:
                    ck(4.4)
                for i in range(nt):
                    if with_q:
                        ob = po_ring.next()
                        for j in range(4):
                            pj = ph[j]
                            mm(pb[ob][:, j * 128:(j + 1) * 128], pj["sm"][:, i * 128:(i + 1) * 128],
                               v_tok[:, i, j * 128:(j + 1) * 128], j == 0, False, pj["Bsm"] + Bv, [B_pb[ob]], skip=True)
                    for cc in range(4):
                        c = i * 4 + cc
                        for j in range(4):
                            h = hg_real * 4 + j
                            pj = ph[j]
                            if with_q:
                                mm(pb[ob][:, j * 128:(j + 1) * 128], pj["qpad"][:, c, :], S_bf[:, h, :], False, cc == 3,
                                   pj["Bqpad"] + [B_Sbf[h]], [B_pb[ob]], skip=True)
                            ub = 6 + (j % 2)
                            if with_q:
                                kz, Bkz = pj["kdz"][i % 2]
                                if cc == 0:
                                    tt("vector", kz[:, :, :], pj["kd"][:, i:i + 1, :].to_broadcast([128, 4, 128]),
                                       cmask[:, :, :], ALU.mult, pj["Bkd"] + [B_const], Bkz)
                                mm(pb[ub][:, 0:128], kz[:, cc, :], v_tok[:, i, j * 128:(j + 1) * 128], True, True,
                                   Bkz + Bv, [B_pb[ub]])
                            else:
                                mm(pb[ub][:, 0:128], pj["kd"][32 * cc:32 * cc + 32, i, :],
                                   v_tok[32 * cc:32 * cc + 32, i, j * 128:(j + 1) * 128], True, True,
                                   pj["Bkd"] + Bv, [B_pb[ub]], tp=(32 * cc, 0))
                            stt("vector", S32[:, h, :], S32[:, h, :], pj["ebl"][:, c:c + 1], pb[ub][:, 0:128],
                                ALU.mult, ALU.add, [B_S32[h], B_pb[ub]] + pj["Bebl"], [B_S32[h]])
                            if with_q:
                                cp("vector", S_bf[:, h, :], S32[:, h, :], [B_S32[h]], [B_Sbf[h]])
                    if with_q:
                        cp("vector", o_sb[:, i, :], pb[ob][:, :], [B_pb[ob]], Bo)
                        act(t1[:, :], pb[ob][:, :], AF.Square, [B_pb[ob]], Bt1)
                        P.op("vector", lambda e, i=i: e.tensor_reduce(out=hst[:, i * 4:i * 4 + 4],
                                                                        in_=t1[:, :].rearrange("p (a b) -> p a b", b=128),
                                                                        axis=AX.X, op=ALU.add),
                             reads=Bt1, writes=Bhst)
                if with_q:
                    ck(4.6)
                if with_q and DBGTAP and nt == 4 and hg == 0 and not hasattr(hgrn_group, "tapped"):
                    hgrn_group.tapped = True
                    dma(dbg_o, o_sb, Bo, [B_out], key="tapo")
                if with_q:
                    n4 = nt * 4
                    ts("vector", hst[:, 32:32 + n4], hst[:, 0:n4], 1.0 / 128.0, RMS_EPS, ALU.mult, ALU.add, Bhst, Bhst)
                    act(hst[:, 32:32 + n4], hst[:, 32:32 + n4], AF.Ln, Bhst, Bhst)
                    act(hst[:, 32:32 + n4], hst[:, 32:32 + n4], AF.Exp, Bhst, Bhst, scale=-0.5)
                    ck(4.7)
                    for i in range(nt):
                        tt("vector", t1[:, :].rearrange("p (a b) -> p a b", b=128),
                           o_sb[:, i, :].rearrange("p (a b) -> p a b", b=128),
                           hst[:, 32 + i * 4:32 + i * 4 + 4].unsqueeze(2).to_broadcast([128, 4, 128]), ALU.mult,
                           Bo + Bhst, Bt1)
                        tt("vector", gated_l[hg][0][:, i, :], t1[:, :], gs[:, i, :], ALU.mult,
                           Bt1 + Bgs, gated_l[hg][1])
                    ck(4.8)
            if with_q:
                class _G:
                    def __getitem__(self, key):
                        _, i_, cs = key
                        hg_ = cs.start // 512
                        return gated_l[hg_][0][:, i_, cs.start - hg_ * 512:cs.stop - hg_ * 512]
                return _G(), gated_l[0][1] + gated_l[1][1]
            return None, None

        hgrn_group.first_q = True
        B_pU = [Buf("pU%d" % j) for j in range(4)]

        def load_lnp(idx):
            dma(lnp[:, 0, :], lnp_d[2 * idx, :].partition_broadcast(128), [], [B_lnp], key="lnpg")
            dma(lnp[:, 1, :], lnp_d[2 * idx + 1, :].partition_broadcast(128), [], [B_lnp], key="lnpb")

        def ln_finalize(nt, to_hT_after, out_tiles=None):
            for i in range(nt):
                P.op("vector", lambda e, i=i: e.bn_aggr(out=mv[:, i, :], in_=bnst[:, i, :, :].rearrange("p a b -> p (a b)")),
                     reads=[B_bnst[i]], writes=[B_mv])
            ts("vector", lnr[:, :nt], mv[:, :nt, 1], LN_EPS, None, ALU.add, None, [B_mv], [B_mv])
            act(lnr[:, :nt], lnr[:, :nt], AF.Ln, [B_mv], [B_mv])
            act(lnr[:, :nt], lnr[:, :nt], AF.Exp, [B_mv], [B_mv], scale=-0.5)
            for i in range(nt):
                ts("vector", h_res[:, i, :], h_res[:, i, :], mv[:, i, 0:1], lnr[:, i:i + 1], ALU.subtract, ALU.mult,
                   [B_h[i], B_mv], [B_h[i]])
                tt("vector", h_res[:, i, :], h_res[:, i, :], lnp[:, 0, :], ALU.mult, [B_h[i], B_lnp], [B_h[i]])
                tt("vector", h_res[:, i, :], h_res[:, i, :], lnp[:, 1, :], ALU.add, [B_h[i], B_lnp], [B_h[i]])
                if to_hT_after:
                    s = xb_ring.next()
                    act(xb[s][:], h_res[:, i, :], AF.Copy, [B_h[i]], [B_xb[s]])
                    to_hT(i, xb[s], B_xb[s], hT, [B_hT[i]])
                if out_tiles is not None and out_tiles[i] is not None:
                    r0 = out_tiles[i] * 128
                    dma(out_d[r0:r0 + 128, :], h_res[:, i, :], [B_h[i]], [B_out], key="out%d" % i)

        B_out = Buf("out")

        def residual_half(i, half, bank):
            cols = slice(half * 512, (half + 1) * 512)
            stt("vector", h_res[:, i, cols], h_res[:, i, cols], ALPHA, pb[bank][:, :], ALU.mult, ALU.add,
                [B_h[i], B_pb[bank]], [B_h[i]])
            P.op("vector", lambda e, i=i, half=half, cols=cols: e.bn_stats(out=bnst[:, i, half, :], in_=h_res[:, i, cols]),
                 reads=[B_h[i]], writes=[B_bnst[i]])

        def mixer_out(nt, src_tok, Bsrc, wb_w, wbuf, ln_idx):
            load_lnp(ln_idx)
            for i in range(nt):
                pt = pbT(PT_BANK)
                for kc in range(8):
                    tr(pt[:, kc, :], src_tok[:, i, kc * 128:(kc + 1) * 128], list(Bsrc), [B_pb[PT_BANK]])
                act(gT[:, 0:8, i * 128:(i + 1) * 128], pt, AF.Copy, [B_pb[PT_BANK]], B_gT[0:8])
            for half in range(2):
                Wo, BWo = load_piece(wb_w[:, half * 512:(half + 1) * 512], 8, 512, wbuf)
                for i in range(nt):
                    b_ = pf_ring.next()
                    for kc in range(8):
                        mm(pb[b_][:, :], gT[:, kc, i * 128:(i + 1) * 128], Wo[:, kc, :], kc == 0, kc == 7,
                           B_gT[0:8] + [BWo], [B_pb[b_]])
                    residual_half(i, half, b_)
            ln_finalize(nt, True)

        def ffn(l, nt, ln_idx, last, t0):
            N = nt * 128
            ar.reset()
            ue_l = [ar.alloc([128, 516], F32) for _ in range(2)]
            yb_l = [ar.alloc([128, 512], F32) for _ in range(2)]
            sa_l = [ar.alloc([128, 512], F32) for _ in range(2)]
            ue_r = Ring(ue_l)
            yb_r = Ring(yb_l)
            load_lnp(ln_idx)
            for p in range(11):
                Wp, BWp = load_piece(wb_fin[l, :, p, :], 8, 512, B_wb["fin%d" % l])
                for blk in range(4):
                    j = 2 * p + (blk % 2)
                    is_b = blk >= 2
                    cn = j + (22 if is_b else 0)
                    b_ = pf_ring.next()
                    for kc in range(8):
                        mm(pb[b_][:, :N], Wp[:, kc, blk * 128:(blk + 1) * 128], hT[:, kc, :N], kc == 0, kc == 7,
                           B_hT[:nt] + [BWp], [B_pb[b_]])
                    ue, Bue = ue_r.next()
                    yb, Byb = yb_r.next()
                    act(ue[:, 0:2], halo[:, l, cn, :], AF.Copy, [B_halo[l]], Bue)
                    act(ue[:, 2:2 + N], pb[b_][:, :N], AF.Copy, [B_pb[b_]], Bue)
                    act(yb[:, :N], pb[b_][:, :N], AF.Identity, [B_pb[b_], B_const], Byb,
                        scale=cw[:, l, cn, 2:3], bias=cw[:, l, cn, 3:4])
                    act(halo[:, l, cn, :], ue[:, N:N + 2], AF.Copy, Bue, [B_halo[l]])
                    stt("vector", yb[:, :N], ue[:, 1:1 + N], cw[:, l, cn, 1:2], yb[:, :N], ALU.mult, ALU.add,
                        Bue + Byb + [B_const], Byb)
                    stt("vector", yb[:, :N], ue[:, 0:N], cw[:, l, cn, 0:1], yb[:, :N], ALU.mult, ALU.add,
                        Bue + Byb + [B_const], Byb)
                    if not is_b:
                        sa, Bsa = sa_l[blk]
                        act(sa[:, :N], yb[:, :N], AF.Silu, Byb, Bsa)
                    else:
                        sa, Bsa = sa_l[blk - 2]
                        tt("vector", gT[:, j, :N], sa[:, :N], yb[:, :N], ALU.mult, Bsa + Byb, [B_gT[j]])
            for half in range(2):
                banks = [0, 1, 2, 3][:nt]
                for pc in range(3):
                    nk = 8 if pc < 2 else 6
                    Wd, BWd = load_piece(wb_fout[l, half, pc * 1024:pc * 1024 + nk * 128, :], nk, 512, B_wb["fout%d" % l])
                    for kk in range(nk):
                        j = pc * 8 + kk
                        for i in range(nt):
                            mm(pb[banks[i]][:, :], gT[:, j, i * 128:(i + 1) * 128], Wd[:, kk, :], j == 0, j == 21,
                               [B_gT[j], BWd], [B_pb[banks[i]]])
                for i in range(nt):
                    residual_half(i, half, banks[i])
            if last:
                ot = [(t0 + i - NW) if (t0 + i) >= NW else None for i in range(nt)]
                ln_finalize(nt, False, ot)
            else:
                ln_finalize(nt, True)

        def swa(nt, t0):
            N = nt * 128
            ar.reset()
            QT, BQT = ar.alloc([128, 8, 512], BF16)
            lg, Blg = ar.alloc([128, 8, 256], F32)
            Pm, BPm = ar.alloc([128, 8, 256], BF16)
            PTs = [ar.alloc([128, 4, 2, 128], BF16) for _ in range(2)]
            PT_r = Ring(PTs)
            attn, Battn = ar.alloc([128, 4, D], BF16)
            stt_, Bst = small[:, 64:160], [B_ast]
            mx, nmx, es, rs, den, rden = (stt_[:, k * 16:(k + 1) * 16] for k in range(6))
            kvK, BkvK = load_piece(wb_kv[:, 0:512], 8, 512, B_wb["kv"])
            kvV, BkvV = load_piece(wb_kv[:, 512:768], 8, 256, B_wb["kv"])
            for ch in range(4):
                b_ = pf_ring.next()
                for kc in range(8):
                    mm(pb[b_][:, :N], kvK[:, kc, ch * 128:(ch + 1) * 128], hT[:, kc, :N], kc == 0, kc == 7,
                       B_hT[:nt] + [BkvK], [B_pb[b_]])
                act(KT[:, ch, 128:128 + N], pb[b_][:, :N], AF.Copy, [B_pb[b_]], [B_KT])
            for i in range(nt):
                b_ = pf_ring.next()
                for kc in range(8):
                    mm(pb[b_][:, :256], hT[:, kc, i * 128:(i + 1) * 128], kvV[:, kc, :], kc == 0, kc == 7,
                       [B_hT[i], BkvV], [B_pb[b_]])
                act(V_tok[:, 1 + i, :], pb[b_][:, :256], AF.Copy, [B_pb[b_]], [B_V])
            for half in range(2):
                Wq_, BWq_ = load_piece(wb_q[:, half * 512:(half + 1) * 512], 8, 512, B_wb["q"])
                for cq in range(4):
                    b_ = pf_ring.next()
                    for kc in range(8):
                        mm(pb[b_][:, :N], Wq_[:, kc, cq * 128:(cq + 1) * 128], hT[:, kc, :N], kc == 0, kc == 7,
                           B_hT[:nt] + [BWq_], [B_pb[b_]])
                    act(QT[:, half * 4 + cq, :N], pb[b_][:, :N], AF.Copy, [B_pb[b_]], BQT, scale=0.125)
            for i in range(nt):
                first_real = (t0 + i == NW)
                for hh in range(2):
                    for pp in range(2):
                        for pq in range(2):
                            pair = pp * 2 + pq
                            c = hh * 4 + pair
                            for e_ in range(2):
                                h = 2 * c + e_
                                base = 64 * e_
                                g = h // 4
                                arr = (g // 2) if (g % 2 == e_) else 2 + g // 2
                                lb_ = [5, 7][e_]
                                mm(pb[lb_][:, pq * 256:(pq + 1) * 256], QT[base:base + 64, c, i * 128:(i + 1) * 128],
                                   KT[base:base + 64, arr, i * 128:i * 128 + 256], True, True, BQT + [B_KT], [B_pb[lb_]])
                        for e_ in range(2):
                            lb_ = [5, 7][e_]
                            k0 = pp * 4 + e_
                            h0 = hh * 8 + k0
                            for pq in range(2):
                                tt("vector", lg[:, k0 + 2 * pq, :], pb[lb_][:, pq * 256:(pq + 1) * 256],
                                   bias_tab[:, h0 + 2 * pq, :], ALU.add, [B_pb[lb_], B_bias], Blg)
                    hs = slice(hh * 8, hh * 8 + 8)
                    if first_real:
                        ts("vector", lg[:, :, 0:128], lg[:, :, 0:128], negflag[:, 0:1], None, ALU.add, None,
                           Blg + [B_const], Blg)
                    P.op("vector", lambda e, hs=hs: e.tensor_reduce(out=mx[:, hs], in_=lg[:, :, :], axis=AX.X, op=ALU.max),
                         reads=Blg, writes=Bst)
                    tt("vector", mx[:, hs], mx[:, hs], sink_bc[:, hs], ALU.max, Bst + [B_const], Bst)
                    ts("vector", nmx[:, hs], mx[:, hs], -1.0, None, ALU.mult, None, Bst, Bst)
                    tt("vector", es[:, hs], sink_bc[:, hs], nmx[:, hs], ALU.add, Bst + [B_const], Bst)
                    act(es[:, hs], es[:, hs], AF.Exp, Bst, Bst)
                    for k in range(8):
                        h = hh * 8 + k
                        act(Pm[:, k, :], lg[:, k, :], AF.Exp, Blg + Bst, BPm, bias=nmx[:, h:h + 1])
                    P.op("vector", lambda e, hs=hs: e.tensor_reduce(out=rs[:, hs], in_=Pm[:, :, :], axis=AX.X, op=ALU.add),
                         reads=BPm, writes=Bst)
                    tt("vector", den[:, hs], rs[:, hs], es[:, hs], ALU.add, Bst, Bst)
                    P.op("vector", lambda e, hs=hs: e.reciprocal(out=rden[:, hs], in_=den[:, hs]), reads=Bst, writes=Bst)
                    pvb = po_ring.next()
                    for q4 in range(2):
                        pt = pb[PT_BANK][:].bitcast(BF16).rearrange("p (a b c) -> p a b c", a=4, b=2)
                        for kk in range(4):
                            k = q4 * 4 + kk
                            for blk in range(2):
                                tr(pt[:, kk, blk, :], Pm[:, k, blk * 128:(blk + 1) * 128], BPm, [B_pb[PT_BANK]])
                        PTv, BPT = PT_r.next()
                        act(PTv[:], pt, AF.Copy, [B_pb[PT_BANK]], BPT)
                        for kk in range(4):
                            k = q4 * 4 + kk
                            h = hh * 8 + k
                            g = h // 4
                            col = k * 64
                            mm(pb[pvb][:, col:col + 64], PTv[:, kk, 0, :], V_tok[:, i, g * 64:(g + 1) * 64],
                               k == 0, False, BPT + [B_V], [B_pb[pvb]], skip=True)
                            mm(pb[pvb][:, col:col + 64], PTv[:, kk, 1, :], V_tok[:, i + 1, g * 64:(g + 1) * 64],
                               False, True, BPT + [B_V], [B_pb[pvb]], skip=True)
                    tt("vector", attn[:, i, hh * 512:(hh + 1) * 512].rearrange("p (a b) -> p a b", b=64),
                       pb[pvb][:, :].rearrange("p (a b) -> p a b", b=64),
                       rden[:, hs].unsqueeze(2).to_broadcast([128, 8, 64]), ALU.mult, [B_pb[pvb]] + Bst, Battn)
            cp("vector", KT[:, :, 0:128], KT[:, :, N:N + 128], [B_KT], [B_KT])
            cp("vector", V_tok[:, 0, :], V_tok[:, nt, :], [B_V], [B_V])
            return attn, Battn

        import os
        KSTOP = float(os.environ.get("KSTOP", "999"))

        class _Stop(Exception):
            pass

        def ck(n):
            if KSTOP <= n:
                raise _Stop()

        def tap(gi, k, nt):
            if DBGTAP and gi == 1:
                for i in range(nt):
                    dma(dbg_d[k, i, :, :], h_res[:, i, :], [B_h[i]], [B_out], key="tap%d_%d" % (k, i))

        taps_done = set()

        def tapbuf(name, ap, bufs, shape, dt):
            if not DBGTAP or name in taps_done:
                return
            taps_done.add(name)
            d_ = nc.dram_tensor(name, list(shape), dt, kind="ExternalOutput").ap()
            dma(d_, ap, list(bufs), [B_out], key="tb_" + name)

        def main_program():
            ck(0)
            conv_some(8)
            ck(1)
            for gi, (t0, nt) in enumerate(pre_groups):
                load_x(x_pre, t0, nt)
                ck(2)
                hgrn_group(nt, False)
                ck(3)
                conv_some(24)
            while conv_some(16):
                pass
            ck(4)
            P.op("gpsimd", lambda e: e.memset(junk_t[:, 0:8], 0.0), reads=[], writes=B_stg + B_stb + B_gT + [B_lnp, B_junk])
            for h in range(8):
                cp("gpsimd", S_bf[:, h, :], S32[:, h, :], [B_S32[h]], [B_Sbf[h]])
            for gi, (t0, nt) in enumerate(groups):
                load_x(x_main, t0, nt)
                gated, Bgated = hgrn_group(nt, True)
                ck(5)
                mixer_out(nt, gated, Bgated, wb_hout, B_wb["hout"], 0)
                tap(gi, 0, nt)
                ck(6)
                ffn(0, nt, 1, False, t0)
                tap(gi, 1, nt)
                ck(7)
                attn, Battn = swa(nt, t0)
                ck(8)
                mixer_out(nt, attn, Battn, wb_o, B_wb["o"], 2)
                tap(gi, 2, nt)
                ffn(1, nt, 3, True, t0)
                ck(9)
                if gi == 0:
                    for l in range(2):
                        ts("vector", halo[:, l, :, :], halo[:, l, :, :], flag[:, 0:1], None, ALU.mult, None,
                           [B_halo[l], B_const], [B_halo[l]])

        try:
            main_program()
        except _Stop:
            pass
        fin = P.op("sync", None, reads=[B_out])
        lastd = {}
        for e_ in ENGS:
            for ins_ in P.streams[e_]:
                if ins_.is_dma:
                    lastd[ins_.key] = ins_
        for ins_ in lastd.values():
            if ins_ not in fin.deps:
                fin.deps.append(ins_)
        for e_ in ENGS:
            if e_ != "sync" and P.streams[e_]:
                cand = [i_ for i_ in P.streams[e_] if i_.fn is not None and not i_.is_dma]
                if cand and cand[-1] not in fin.deps:
                    fin.deps.append(cand[-1])
        stats = P.emit(st)
        if dbg:
            print("instr/wait counts:", stats)
    return nc


def _t5_bucket_np(dist):
    exact = 16
    d = np.maximum(dist, 1).astype(np.float32)
    val = (np.log(d / exact) / math.log(128 / exact) * (32 - exact))
    import os
    if os.environ.get("BUCKET_ROUND") == "1":
        val = np.rint(val)
    log_b = exact + val.astype(np.int32)
    return np.where(dist < exact, dist, np.minimum(log_b, 31))


_CACHE = {}


def kernel(x, hgrn_w_in, hgrn_lb_logits, hgrn_gnorm_w, hgrn_w_out, swa_w_q, swa_sinks, swa_w_out,
           shared_w_kv, rel_bias, ffn_w_in, ffn_conv_w, ffn_conv_b, ffn_w_out,
           ln_mix_g, ln_mix_b, ln_ffn_g, ln_ffn_b, _dbg=False):
    x = np.asarray(x, np.float32)
    Bn, S, _ = x.shape
    Tm = S // 2
    NPRE = (Tm - 256) // 128
    if S not in _CACHE:
        _CACHE[S] = build(S, dbg=_dbg)
    nc = _CACHE[S]
    f32 = lambda a: np.ascontiguousarray(np.asarray(a, np.float32))
    tt_ = np.arange(128)[:, None] + 128
    ss_ = np.arange(256)[None, :]
    dist = tt_ - ss_
    bucket = _t5_bucket_np(np.maximum(dist, 0))
    relg = f32(np.asarray(rel_bias)[bucket].transpose(0, 2, 1))
    maskadd = np.where((dist >= 0) & (dist < 128), 0.0, NEG).astype(np.float32)
    s_i = np.arange(128)[:, None]
    t_i = np.arange(128)[None, :]
    m01 = ((s_i <= t_i) & (s_i // 32 == t_i // 32)).astype(np.float32)
    mask01 = np.tile(m01, (1, 4))
    scanm = np.tile((np.arange(512) % 32 != 0).astype(np.float32)[None, :], (128, 1))
    ident = np.eye(128, dtype=np.float32)
    cmask = np.zeros((128, 4, 128), np.float32)
    for cc_ in range(4):
        cmask[32 * cc_:32 * cc_ + 32, cc_, :] = 1.0
    cmask = cmask.reshape(128, 512)
    lbl = f32(np.asarray(hgrn_lb_logits).reshape(2, 8, 128).transpose(2, 0, 1))
    gw_row = f32(np.tile(np.asarray(hgrn_gnorm_w).reshape(1, 128), (1, 8)))
    lnp = f32(np.stack([ln_mix_g[0], ln_mix_b[0], ln_ffn_g[0], ln_ffn_b[0],
                        ln_mix_g[1], ln_mix_b[1], ln_ffn_g[1], ln_ffn_b[1]]))
    cwb = np.concatenate([np.asarray(ffn_conv_w), np.asarray(ffn_conv_b)[:, None, :]], axis=1)
    cw = f32(cwb.reshape(2, 4, 44, 128).transpose(3, 0, 2, 1))
    sink_row = f32(np.asarray(swa_sinks).reshape(1, 16))
    common = {
        "w_hin": f32(hgrn_w_in[0]), "w_hout": f32(hgrn_w_out[0]), "w_q": f32(swa_w_q[0]), "w_o": f32(swa_w_out[0]),
        "w_kv": f32(shared_w_kv), "w_fin": f32(ffn_w_in), "w_fout": f32(ffn_w_out),
        "lbl": lbl, "gw_row": gw_row, "lnp": lnp, "cw": cw, "sink_row": sink_row, "relg": relg,
        "maskadd": maskadd, "ident": ident, "mask01": mask01, "scanm": scanm, "cmask": cmask,
    }
    in_maps = []
    for c in range(8):
        b, half = c // 2, c % 2
        start = half * Tm
        xm = np.zeros((Tm + 256, D), np.float32)
        xp = np.zeros((max(NPRE, 1) * 128, D), np.float32)
        if half == 0:
            xm[256:] = x[b, 0:Tm]
        else:
            xm[:] = x[b, start - 256:start + Tm]
            if NPRE > 0:
                xp[:] = x[b, 0:start - 256]
        m = dict(common)
        m["x_main"] = xm
        m["x_pre"] = xp
        m["flag"] = np.full((128, 1), float(half), np.float32)
        in_maps.append(m)
    if _dbg == "maps":
        return nc, in_maps
    res = run_bass_kernel_spmd(nc, in_maps, core_ids=list(range(8)))
    out = np.empty((Bn, S, D), np.float32)
    for c in range(8):
        b, half = c // 2, c % 2
        out[b, half * Tm:(half + 1) * Tm] = res.results[c]["out"]
    return out
```
